# Optimizing a Trainium2 kernel written in Bass

```python
import math
import jax, jax.numpy as jnp
from jax import lax
import numpy as np

D_MODEL = 2048
BATCH = 4
SEQ = 8192
DEPTH = 1
DEC_BATCH = 32
DEC_SEQ = 64
PAST_LEN = 2048

CHUNK = 64
D_ATTN = D_MODEL // 2
D_HG = D_MODEL - D_ATTN
HEAD_DIM = 128
N_HEADS = D_ATTN // HEAD_DIM
N_KV = 2
GROUP = N_HEADS // N_KV
WINDOW = 128
N_WIN_CHUNKS = WINDOW // CHUNK
NUM_BUCKETS = 32
MAX_DISTANCE = 128
HG_DK = 128
HG_DV = 128
HG_HEADS = D_HG // HG_DV
HG_BLOCK = 16
D_FF = 5632
PLE_DIM = 256
EPS = 1e-6
NEG_INF = -1e30
IN_SPLITS = (D_ATTN, N_KV * HEAD_DIM, N_KV * HEAD_DIM, HG_HEADS * HG_DK, HG_HEADS * HG_DK, HG_HEADS * HG_DV, HG_HEADS * HG_DV)
D_IN = sum(IN_SPLITS)
IN_OFFSETS = tuple(int(o) for o in np.cumsum(IN_SPLITS)[:-1])

kernel_name = 'hybrid_swa_hgrn2_streaming_step'


def rms_norm(x, gain):
    xf = x.astype(jnp.float32)
    y = xf * lax.rsqrt(jnp.mean(xf * xf, axis=-1, keepdims=True) + EPS)
    return (y * gain.astype(jnp.float32)).astype(x.dtype)


def swiglu_half_step(x, pre, post, w_gate, w_up, w_down):
    h = rms_norm(x, pre)
    y = (jax.nn.silu(h @ w_gate) * (h @ w_up)) @ w_down
    return x + 0.5 * rms_norm(y, post)


def t5_bucket(rel):
    half = NUM_BUCKETS // 2
    max_exact = half // 2
    n = jnp.abs(rel)
    nf = jnp.maximum(n, 1).astype(jnp.float32)
    large = max_exact + (jnp.log(nf / max_exact) / math.log(MAX_DISTANCE / max_exact) * (half - max_exact)).astype(jnp.int32)
    large = jnp.minimum(large, half - 1)
    return jnp.where(rel > 0, half, 0) + jnp.where(n < max_exact, n, large)


def rel_bias(table, n_q, n_k, offset):
    rel = jnp.arange(n_k)[None, :] - offset - jnp.arange(n_q)[:, None]
    return jnp.transpose(table[t5_bucket(rel)], (2, 0, 1)).astype(jnp.float32)


def attend(q, k, v, bias, sinks, key_mask=None):
    b, n, lq = q.shape[:3]
    lk = k.shape[2]
    qg = q.reshape(b, n, lq, N_KV, GROUP, HEAD_DIM)
    s = jnp.einsum('bnqkgd,bnskd->bnkgqs', qg, k).astype(jnp.float32) * (HEAD_DIM ** -0.5)
    s = s + bias.reshape(N_KV, GROUP, lq, lk)
    if key_mask is not None:
        s = jnp.where(key_mask[None, :, None, None, None, :], s, NEG_INF)
    sink = sinks.astype(jnp.float32).reshape(N_KV, GROUP, 1, 1)
    m = jnp.maximum(jnp.max(s, axis=-1, keepdims=True), sink)
    e = jnp.exp(s - m)
    p = e / (jnp.sum(e, axis=-1, keepdims=True) + jnp.exp(sink - m))
    o = jnp.einsum('bnkgqs,bnskd->bnqkgd', p.astype(v.dtype), v)
    return o.reshape(b, n, lq, N_HEADS * HEAD_DIM)


def swa_prompt(q, k, v, table, sinks):
    b, t = q.shape[:2]
    nc = t // CHUNK
    lk = (N_WIN_CHUNKS + 1) * CHUNK
    qb = q.reshape(b, nc, CHUNK, N_HEADS, HEAD_DIM)

    def band(a):
        ap = jnp.pad(a, ((0, 0), (WINDOW, 0), (0, 0), (0, 0))).reshape(b, nc + N_WIN_CHUNKS, CHUNK, N_KV, HEAD_DIM)
        return jnp.concatenate([ap[:, j:j + nc] for j in range(N_WIN_CHUNKS + 1)], axis=2)

    key_pos = jnp.arange(nc)[:, None] * CHUNK - WINDOW + jnp.arange(lk)[None, :]
    o = attend(qb, band(k), band(v), rel_bias(table, CHUNK, lk, WINDOW), sinks, key_pos >= 0)
    return o.reshape(b, t, N_HEADS * HEAD_DIM)


def swa_sample(q, k, v, k_cache, v_cache, table, sinks):
    b, t = q.shape[:2]
    wc = k_cache.shape[1]
    k_all = jnp.concatenate([k_cache.astype(k.dtype), k], axis=1)
    v_all = jnp.concatenate([v_cache.astype(v.dtype), v], axis=1)
    o = attend(q[:, None], k_all[:, None], v_all[:, None], rel_bias(table, t, wc + t, wc), sinks)
    return o.reshape(b, t, N_HEADS * HEAD_DIM), k_all[:, t:], v_all[:, t:]


def hgrn2_mixer(q, f_logit, i, g, lb, norm_gain, s0):
    b, t = q.shape[:2]
    f32 = jnp.float32
    lbf = lb.astype(f32)
    f = lbf + (1.0 - lbf) * jax.nn.sigmoid(f_logit.astype(f32))
    n_blk = -(-t // HG_BLOCK)
    pad = n_blk * HG_BLOCK - t

    def blocks(a, d):
        a = jnp.pad(a.astype(f32), ((0, 0), (0, pad), (0, 0))).reshape(b, n_blk, HG_BLOCK, HG_HEADS, d)
        return jnp.transpose(a, (1, 0, 3, 2, 4))

    qb = blocks(q, HG_DK)
    log_f = blocks(jnp.log(f), HG_DK)
    kb = blocks(1.0 - f, HG_DK)
    ib = blocks(i, HG_DV)
    cum = jnp.cumsum(log_f, axis=3)
    q_dec = qb * jnp.exp(cum)
    k_inv = kb * jnp.exp(-cum)
    k_end = kb * jnp.exp(cum[:, :, :, -1:, :] - cum)
    blk_decay = jnp.exp(cum[:, :, :, -1, :])
    causal = jnp.tril(jnp.ones((HG_BLOCK, HG_BLOCK), dtype=bool))
    a = jnp.where(causal, jnp.einsum('nbhtk,nbhsk->nbhts', q_dec, k_inv), 0.0)
    intra = jnp.einsum('nbhts,nbhsv->nbhtv', a, ib)

    def step(state, xs):
        q_d, k_e, iv, dec = xs
        out = jnp.einsum('bhtk,bhkv->bhtv', q_d, state)
        state = dec[..., None] * state + jnp.einsum('bhtk,bhtv->bhkv', k_e, iv)
        return state, out

    s_final, inter = lax.scan(step, s0.astype(f32), (q_dec, k_end, ib, blk_decay))
    o = jnp.transpose(intra + inter, (1, 0, 3, 2, 4)).reshape(b, n_blk * HG_BLOCK, HG_HEADS, HG_DV)[:, :t]
    o = rms_norm(o, norm_gain) * jax.nn.silu(g.astype(f32).reshape(b, t, HG_HEADS, HG_DV))
    return o.reshape(b, t, D_HG).astype(q.dtype), s_final.astype(s0.dtype)


def trunk_layer(x, p, lb, lp, table, s0, kv_cache):
    x = swiglu_half_step(x, lp['ffn1_pre'], lp['ffn1_post'], lp['ffn1_w_gate'], lp['ffn1_w_up'], lp['ffn1_w_down'])
    b, t = x.shape[:2]
    z = rms_norm(x, lp['mix_pre']) @ lp['w_in']
    qa, ka, va, qh, fh, ih, gh = jnp.split(z, IN_OFFSETS, axis=-1)
    qa = qa.reshape(b, t, N_HEADS, HEAD_DIM)
    ka = ka.reshape(b, t, N_KV, HEAD_DIM)
    va = va.reshape(b, t, N_KV, HEAD_DIM)
    if kv_cache is None:
        attn = swa_prompt(qa, ka, va, table, lp['attn_sinks'])
        keep = min(WINDOW, t)
        k_new, v_new = ka[:, t - keep:], va[:, t - keep:]
    else:
        attn, k_new, v_new = swa_sample(qa, ka, va, kv_cache[0], kv_cache[1], table, lp['attn_sinks'])
    rec, s_new = hgrn2_mixer(qh, fh, ih, gh, lb, lp['hgrn_norm'], s0)
    mix = jnp.concatenate([attn, rec], axis=-1) @ lp['w_out']
    x = x + rms_norm(mix, lp['mix_post'])
    x = swiglu_half_step(x, lp['ffn2_pre'], lp['ffn2_post'], lp['ffn2_w_gate'], lp['ffn2_w_up'], lp['ffn2_w_down'])
    h = rms_norm(x, lp['ple_pre'])
    y = jax.nn.sigmoid(h @ lp['w_ple_gate']) * (p @ lp['w_ple_proj'])
    x = x + rms_norm(y, lp['ple_post'])
    return x, k_new, v_new, s_new


def setup_inputs(seed: int = 0) -> dict:
    key = jax.random.key(seed)
    ks = jax.random.split(key, 32)
    f32 = jnp.float32

    def nrm(k, shape, scale):
        return jax.random.normal(k, shape, f32) * scale

    def gain(k, shape):
        return 1.0 + 0.05 * jax.random.normal(k, shape, f32)

    wc = min(WINDOW, PAST_LEN)
    return {
        'x_prompt': nrm(ks[0], (BATCH, SEQ, D_MODEL), 1.0),
        'x_sample': nrm(ks[1], (DEC_BATCH, DEC_SEQ, D_MODEL), 1.0),
        'cache_attn_k': nrm(ks[2], (DEPTH, DEC_BATCH, wc, N_KV, HEAD_DIM), 1.0),
        'cache_attn_v': nrm(ks[3], (DEPTH, DEC_BATCH, wc, N_KV, HEAD_DIM), 1.0),
        'state_hgrn': nrm(ks[4], (DEPTH, DEC_BATCH, HG_HEADS, HG_DK, HG_DV), 0.1),
        'p_prompt': nrm(ks[5], (DEPTH, BATCH, SEQ, PLE_DIM), 1.0),
        'p_sample': nrm(ks[6], (DEPTH, DEC_BATCH, DEC_SEQ, PLE_DIM), 1.0),
        'rel_bias_table': nrm(ks[7], (NUM_BUCKETS, N_HEADS), 0.5),
        'ffn1_pre': gain(ks[8], (DEPTH, D_MODEL)),
        'ffn1_post': gain(ks[9], (DEPTH, D_MODEL)),
        'ffn1_w_gate': nrm(ks[10], (DEPTH, D_MODEL, D_FF), D_MODEL ** -0.5),
        'ffn1_w_up': nrm(ks[11], (DEPTH, D_MODEL, D_FF), D_MODEL ** -0.5),
        'ffn1_w_down': nrm(ks[12], (DEPTH, D_FF, D_MODEL), D_FF ** -0.5),
        'mix_pre': gain(ks[13], (DEPTH, D_MODEL)),
        'mix_post': gain(ks[14], (DEPTH, D_MODEL)),
        'w_in': nrm(ks[15], (DEPTH, D_MODEL, D_IN), D_MODEL ** -0.5),
        'w_out': nrm(ks[16], (DEPTH, D_MODEL, D_MODEL), D_MODEL ** -0.5),
        'attn_sinks': nrm(ks[17], (DEPTH, N_HEADS), 0.5),
        'hgrn_lb_logits': nrm(ks[18], (DEPTH + 1, D_HG), 0.5),
        'hgrn_norm': gain(ks[19], (DEPTH, HG_DV)),
        'ffn2_pre': gain(ks[20], (DEPTH, D_MODEL)),
        'ffn2_post': gain(ks[21], (DEPTH, D_MODEL)),
        'ffn2_w_gate': nrm(ks[22], (DEPTH, D_MODEL, D_FF), D_MODEL ** -0.5),
        'ffn2_w_up': nrm(ks[23], (DEPTH, D_MODEL, D_FF), D_MODEL ** -0.5),
        'ffn2_w_down': nrm(ks[24], (DEPTH, D_FF, D_MODEL), D_FF ** -0.5),
        'ple_pre': gain(ks[25], (DEPTH, D_MODEL)),
        'ple_post': gain(ks[26], (DEPTH, D_MODEL)),
        'w_ple_gate': nrm(ks[27], (DEPTH, D_MODEL, D_MODEL), D_MODEL ** -0.5),
        'w_ple_proj': nrm(ks[28], (DEPTH, PLE_DIM, D_MODEL), PLE_DIM ** -0.5),
    }


def reference(x_prompt, x_sample, cache_attn_k, cache_attn_v, state_hgrn, p_prompt, p_sample,
              rel_bias_table, ffn1_pre, ffn1_post, ffn1_w_gate, ffn1_w_up, ffn1_w_down,
              mix_pre, mix_post, w_in, w_out, attn_sinks, hgrn_lb_logits, hgrn_norm,
              ffn2_pre, ffn2_post, ffn2_w_gate, ffn2_w_up, ffn2_w_down,
              ple_pre, ple_post, w_ple_gate, w_ple_proj):
    lower_bounds = jnp.cumsum(jax.nn.softmax(hgrn_lb_logits.astype(jnp.float32), axis=0), axis=0)
    yp, ys = x_prompt, x_sample
    kp_l, vp_l, sp_l, ks_l, vs_l, ss_l = [], [], [], [], [], []
    for l in range(DEPTH):
        lp = {
            'ffn1_pre': ffn1_pre[l], 'ffn1_post': ffn1_post[l], 'ffn1_w_gate': ffn1_w_gate[l],
            'ffn1_w_up': ffn1_w_up[l], 'ffn1_w_down': ffn1_w_down[l],
            'mix_pre': mix_pre[l], 'mix_post': mix_post[l], 'w_in': w_in[l], 'w_out': w_out[l],
            'attn_sinks': attn_sinks[l], 'hgrn_norm': hgrn_norm[l],
            'ffn2_pre': ffn2_pre[l], 'ffn2_post': ffn2_post[l], 'ffn2_w_gate': ffn2_w_gate[l],
            'ffn2_w_up': ffn2_w_up[l], 'ffn2_w_down': ffn2_w_down[l],
            'ple_pre': ple_pre[l], 'ple_post': ple_post[l], 'w_ple_gate': w_ple_gate[l], 'w_ple_proj': w_ple_proj[l],
        }
        s0_prompt = jnp.zeros((yp.shape[0], HG_HEADS, HG_DK, HG_DV), state_hgrn.dtype)
        yp, kp, vp, sp = trunk_layer(yp, p_prompt[l], lower_bounds[l], lp, rel_bias_table, s0_prompt, None)
        ys, kn, vn, sn = trunk_layer(ys, p_sample[l], lower_bounds[l], lp, rel_bias_table, state_hgrn[l],
                                     (cache_attn_k[l], cache_attn_v[l]))
        kp_l.append(kp); vp_l.append(vp); sp_l.append(sp)
        ks_l.append(kn); vs_l.append(vn); ss_l.append(sn)
    return (yp, ys, jnp.stack(kp_l), jnp.stack(vp_l), jnp.stack(sp_l), jnp.stack(ks_l), jnp.stack(vs_l), jnp.stack(ss_l))
```

```python
import contextlib
import math
import numpy as np
import concourse.bass as bass
import concourse.mybir as mybir
from concourse.bass_utils import run_bass_kernel_spmd

F32 = mybir.dt.float32
BF16 = mybir.dt.bfloat16
AF = mybir.ActivationFunctionType
ALU = mybir.AluOpType

D = 2048
KC = 16
FF = 5632
FC = 44
T = 256
NB = 2
NCH = 4
NBLK = 16
EPS = 1e-6
NSLOT = 4
SLOT_ELEMS = 4096
SAME_ENG_SYNC = ("act", "dve", "pool")
STAGES = ("ffn1", "mixer", "ffn2", "ple")
SAMPLE = True
NCORES = 8
DEBUG_SCHED = False
DEBUG_MIX = False
MIX_PARTS = ("attn", "hgrn", "wout")


class Op:
    __slots__ = ("eng", "fn", "deps", "dma", "signal", "token", "idx")


class Sched:
    def __init__(self):
        self.ops = []
        self.st = {}
        self.last_dma = {}

    def _deps_for(self, key, write, opi, deps):
        name, idx = key
        st = self.st.setdefault(name, {"ww": None, "wi": {}, "rw": [], "ri": {}})
        if idx is None:
            if st["ww"] is not None:
                deps.add(st["ww"])
            deps.update(st["wi"].values())
            if write:
                deps.update(st["rw"])
                for l in st["ri"].values():
                    deps.update(l)
        else:
            if st["ww"] is not None:
                deps.add(st["ww"])
            if idx in st["wi"]:
                deps.add(st["wi"][idx])
            if write:
                deps.update(st["rw"])
                deps.update(st["ri"].get(idx, ()))

    def _commit(self, key, write, opi):
        name, idx = key
        st = self.st[name]
        op = self.ops[opi]

        def addr(lst):
            if not op.dma:
                lst[:] = [o for o in lst if self.ops[o].dma or self.ops[o].eng != op.eng]
            lst.append(opi)

        if idx is None:
            if write:
                st["ww"] = opi
                st["wi"] = {}
                st["rw"] = []
                st["ri"] = {}
            else:
                addr(st["rw"])
        else:
            if write:
                st["wi"][idx] = opi
                st["ri"][idx] = []
            else:
                addr(st["ri"].setdefault(idx, []))

    def add(self, eng, fn, reads=(), writes=(), dma=None, extra_deps=()):
        op = Op()
        op.eng, op.fn, op.dma, op.signal, op.token = eng, fn, dma, False, None
        opi = len(self.ops)
        op.idx = opi
        deps = set(extra_deps)
        writes = list(writes) + [k for k in reads if k[0] == "ps"]
        reads = [k for k in reads if k[0] != "ps"]
        for k in reads:
            self._deps_for(k, False, opi, deps)
        for k in writes:
            self._deps_for(k, True, opi, deps)
        if dma is not None and dma in self.last_dma:
            deps.add(self.last_dma[dma])
        deps.discard(opi)
        op.deps = deps
        self.ops.append(op)
        if dma is not None:
            self.last_dma[dma] = opi
        for k in reads:
            self._commit(k, False, opi)
        for k in writes:
            self._commit(k, True, opi)
        return opi

    def check_deadlock(self):
        ops = self.ops
        streams = {}
        for op in ops:
            streams.setdefault(op.eng, []).append(op)
        pos = {e: 0 for e in streams}
        sem = {}
        progress = True
        while progress:
            progress = False
            for e, lst in streams.items():
                while pos[e] < len(lst):
                    op = lst[pos[e]]
                    ok = True
                    for d in op.deps:
                        dop = ops[d]
                        if dop.token is None:
                            continue
                        if dop.dma is None and dop.eng == e and e not in SAME_ENG_SYNC:
                            continue
                        k, v = dop.token
                        if sem.get(k, 0) < v:
                            ok = False
                            break
                    if not ok:
                        break
                    if op.token is not None:
                        k, v = op.token
                        sem[k] = sem.get(k, 0) + (16 if op.dma is not None else 1)
                        assert sem[k] == v, (k, v, sem[k])
                    pos[e] += 1
                    progress = True
        for e, lst in streams.items():
            if pos[e] < len(lst):
                op = lst[pos[e]]
                raise RuntimeError(f"DEADLOCK: engine {e} stuck at op {op.idx} deps {[(d, ops[d].eng, ops[d].token) for d in op.deps]}")
        print("deadlock check ok", {e: len(l) for e, l in streams.items()}, "sems", len(sem))

    def emit(self, nc, es):
        ops = self.ops
        for op in ops:
            for d in op.deps:
                dop = ops[d]
                if dop.dma is not None:
                    dop.signal = True
                elif dop.eng != op.eng:
                    dop.signal = True
                elif op.eng in SAME_ENG_SYNC:
                    dop.signal = True
        sems = {}

        def getsem(name):
            if name not in sems:
                sems[name] = es.enter_context(nc.semaphore("s_" + name))
            return sems[name]

        cnt = {}
        for op in ops:
            if op.dma is not None:
                k = "d_" + op.dma
                cnt[k] = cnt.get(k, 0) + 16
                op.token = (k, cnt[k])
                getsem(k)
            elif op.signal:
                k = "e_" + op.eng
                cnt[k] = cnt.get(k, 0) + 1
                op.token = (k, cnt[k])
                getsem(k)
        self.check_deadlock()
        block = es.enter_context(nc.Block())
        engs = {"pe": block.tensor, "act": block.scalar, "dve": block.vector, "pool": block.gpsimd, "sp": block.sync}
        for ename, deco in engs.items():
            mine = [op for op in ops if op.eng == ename]

            def body(e, mine=mine, ename=ename):
                waited = {}
                for op in mine:
                    need = {}
                    for d in op.deps:
                        dop = ops[d]
                        if dop.token is None:
                            continue
                        if dop.dma is None and dop.eng == ename and ename not in SAME_ENG_SYNC:
                            continue
                        k, v = dop.token
                        if need.get(k, 0) < v:
                            need[k] = v
                    wl = []
                    for k, v in need.items():
                        if waited.get(k, 0) >= v:
                            continue
                        e.wait_ge(sems[k], v)
                        waited[k] = v
                        wl.append((k, v))
                    if DEBUG_SCHED:
                        print(ename, op.idx, "waits", wl, "token", op.token, getattr(op.fn, "__name__", None))
                    if op.fn is None:
                        continue
                    ins = op.fn(e)
                    if op.token is not None:
                        if op.dma is not None:
                            ins.then_inc(sems[op.token[0]], 16)
                        else:
                            ins.then_inc(sems[op.token[0]], 1)

            deco(body)


W_SPECS = [
    ("w1g", D, FF), ("w1u", D, FF), ("w1d", FF, D), ("win", D, FF), ("wout", D, D),
    ("w2g", D, FF), ("w2u", D, FF), ("w2d", FF, D), ("wpg", D, D), ("wpp", 256, D),
]
OFF_QA, OFF_K, OFF_V, OFF_QH, OFF_FH, OFF_IH, OFF_GH = 0, 1024, 1280, 1536, 2560, 3584, 4608

GC_F1PRE, GC_F1POST, GC_MPRE, GC_MPOST, GC_F2PRE, GC_F2POST, GC_PPRE, GC_PPOST = [16 * i for i in range(8)]
GC_HN, GC_L0, GC_L1, GC_SINK = 128, 129, 137, 145
NGC = 153


def build(NPT, first_tile_is_seq_start=True, stages=None, sample=None, NPRE=0):
    stages = STAGES if stages is None else stages
    sample = SAMPLE if sample is None else sample
    nc = bass.Bass("TRN2", target_bir_lowering=False)
    S = Sched()
    NTOK = NPT * T

    def din(name, shape, dt=F32):
        return nc.dram_tensor(name, list(shape), dt, kind="ExternalInput").ap()

    def dout(name, shape):
        return nc.dram_tensor(name, list(shape), F32, kind="ExternalOutput").ap()

    xp = din("xp", [max(NTOK, 1), D])
    pp = din("pp", [max(NTOK, 1), 256])
    xpre = din("xpre", [max(NPRE * T, 1), D])
    flag_d = din("flag", [128, 1])
    xs = din("xs", [T, D])
    pss = din("ps", [T, 256])
    ck = din("ck", [4, 128, 256])
    cv = din("cv", [4, 128, 256])
    stin = din("st", [4, 8, 128, 128])
    Wd = {n: din(n, [k, m]) for n, k, m in W_SPECS}
    gains_d = din("gains", [128, NGC])
    bias_d = din("biast", [5, 128, 512])
    ident_d = din("ident", [128, 128])
    bmask_d = din("bmask", [128, 128])
    smask_d = din("smask", [128, 4 * T])
    imask_d = din("imask", [128, 4])

    yp = dout("yp", [max(NTOK, 1), D])
    ys = dout("ys", [T, D])
    kp = dout("kp", [128, 256])
    vp = dout("vp", [128, 256])
    spo = dout("spo", [8, 128, 128])
    kso = dout("kso", [4, 128, 256])
    vso = dout("vso", [4, 128, 256])
    sso = dout("sso", [4, 8, 128, 128])
    dbgmix = nc.dram_tensor("dbgmix", [max(NPT, 1), 128, KC * T], BF16, kind="ExternalOutput").ap() if DEBUG_MIX else None

    scr = {}

    def mk_scr(name, nblk, elems):
        scr[name] = nc.dram_tensor("scr_" + name, [nblk, 128, elems], BF16, kind="Internal").ap()

    for f in ("w1", "w2"):
        mk_scr(f + "g", 22, KC * 256)
        mk_scr(f + "u", 22, KC * 256)
        mk_scr(f + "d", 16, FC * 128)
    mk_scr("win", 22, KC * 256)
    mk_scr("wout", 8, KC * 256)
    mk_scr("wpg", 8, KC * 256)
    mk_scr("wpp", 1, 2 * D)

    es = contextlib.ExitStack()
    with es:
        def sb(name, shape, dt=F32):
            return es.enter_context(nc.sbuf_tensor("sb_" + name, list(shape), dt))

        def psb(name):
            return es.enter_context(nc.psum_tensor(name, [128, 512], F32))

        xT = sb("xT", [128, KC, T])
        yT = sb("yT", [128, KC, T])
        hT = sb("hT", [128, KC, T], BF16)
        mixT = sb("mixT", [128, KC, T], BF16)
        big = sb("big", [128, FC * T], BF16)
        stg = sb("stg", [128, NB, D])
        slots = [sb(f"slot{i}", [128, SLOT_ELEMS], BF16) for i in range(NSLOT)]
        gains = sb("gains", [128, NGC])
        ghalf = sb("ghalf", [128, 32])
        lbt = sb("lbt", [128, 8]); oml = sb("oml", [128, 8]); noml = sb("noml", [128, 8]); esink = sb("esink", [128, 8])
        biast = sb("biast", [128, 5, 512])
        ident = sb("ident", [128, 128])
        bmask = sb("bmask", [128, 128])
        smask = sb("smask", [128, 4 * T])
        imask = sb("imask", [128, 4])
        flagt = sb("flagt", [128, 1])
        ones_b = sb("ones_b", [128, 128], BF16)
        ones_f = sb("ones_f", [128, 128])
        rstd = sb("rstd", [128, T])
        sqb = [sb(f"sqb{i}", [128, 4, T], BF16) for i in range(2)]
        tmps = [sb(f"tmps{i}", [128, T]) for i in range(2)]
        pT = sb("pT", [128, 2, T], BF16)
        pstg = sb("pstg", [128, NB, 256])
        kvstg = sb("kvstg", [128, NB, 512])
        qT = sb("qT", [128, 8, T], BF16)
        KT = sb("KT", [128, 2, 3 * 128], BF16)
        Vt = sb("Vt", [128, 3, 256], BF16)
        KTc = sb("KTc", [128, 2, 4 * 128], BF16)
        Vc = sb("Vc", [128, 4, 256], BF16)
        sc = [sb(f"sc{i}", [128, 512]) for i in range(2)]
        PT = [sb(f"PT{i}", [128, 512], BF16) for i in range(4)]
        densb = sb("densb", [128, 512])
        HV = 4 * T * 2
        bigf = big[:, :].bitcast(F32)

        def hview(i):
            return bigf[:, i * 4 * T:(i + 1) * 4 * T]

        hA, hB, hC, hQ, hG = [hview(i) for i in range(5)]

        def hkeys(i):
            return [("big", c) for c in range(i * 8, (i + 1) * 8)]

        itok = sb("itok", [128, NB, 512])
        im4 = sb("im4", [128, 4, 512])
        kend_tok = sb("kend_tok", [128, 512])
        Am = sb("Am", [128, 512])
        sq32 = sb("sq32", [128, 512])
        rs32 = sb("rs32", [128, 512])
        tmpo = sb("tmpo", [128, 512])
        Sring = [sb(f"Sring{i}", [128, 4, 128]) for i in range(4)]
        Scarry = [sb(f"Scarry{i}", [128, 4, 128]) for i in range(2)]
        Dd = sb("Dd", [128, 4, NBLK])
        ps = [psb(f"ps{i}") for i in range(8)]

        def ld(dst, src, key, q="sp"):
            S.add(q, lambda e: e.dma_start(out=dst, in_=src), writes=[key], dma="c_" + key[0])

        ld(gains[:], gains_d[:, :], ("gains", None))
        ld(ident[:], ident_d[:, :], ("ident", None))
        ld(biast[:], bias_d.rearrange("f p n -> p f n"), ("biast", None))
        ld(bmask[:], bmask_d[:, :], ("bmask", None))
        ld(smask[:], smask_d[:, :], ("smask", None))
        ld(imask[:], imask_d[:, :], ("imask", None))
        ld(flagt[:], flag_d[:, :], ("flagt", None))
        S.add("pool", lambda e: e.memset(ones_b[:], 1.0), writes=[("ones_b", None)])
        S.add("pool", lambda e: e.memset(ones_f[:], 1.0), writes=[("ones_f", None)])
        S.add("dve", lambda e: e.tensor_scalar(out=ghalf[:, 0:16], in0=gains[:, GC_F1POST:GC_F1POST + 16], scalar1=0.5, scalar2=None, op0=ALU.mult),
              reads=[("gains", None)], writes=[("ghalf", None)])
        S.add("dve", lambda e: e.tensor_scalar(out=ghalf[:, 16:32], in0=gains[:, GC_F2POST:GC_F2POST + 16], scalar1=0.5, scalar2=None, op0=ALU.mult),
              reads=[("gains", None)], writes=[("ghalf", None)])
        S.add("dve", lambda e: e.tensor_tensor(out=lbt[:], in0=gains[:, GC_L1:GC_L1 + 8], in1=gains[:, GC_L0:GC_L0 + 8], op=ALU.subtract),
              reads=[("gains", None)], writes=[("lbt", None)])
        S.add("act", lambda e: e.activation(out=lbt[:], in_=lbt[:], func=AF.Exp), reads=[("lbt", None)], writes=[("lbt", None)])
        S.add("dve", lambda e: e.tensor_scalar(out=lbt[:], in0=lbt[:], scalar1=1.0, scalar2=None, op0=ALU.add), reads=[("lbt", None)], writes=[("lbt", None)])
        S.add("dve", lambda e: e.reciprocal(out=lbt[:], in_=lbt[:]), reads=[("lbt", None)], writes=[("lbt", None)])
        S.add("dve", lambda e: e.tensor_scalar(out=oml[:], in0=lbt[:], scalar1=-1.0, scalar2=1.0, op0=ALU.mult, op1=ALU.add),
              reads=[("lbt", None)], writes=[("oml", None)])
        S.add("dve", lambda e: e.tensor_scalar(out=noml[:], in0=lbt[:], scalar1=1.0, scalar2=-1.0, op0=ALU.mult, op1=ALU.add),
              reads=[("lbt", None)], writes=[("noml", None)])
        S.add("act", lambda e: e.activation(out=esink[:], in_=gains[:, GC_SINK:GC_SINK + 8], func=AF.Exp), reads=[("gains", None)], writes=[("esink", None)])

        cast_i = [0]

        pending_casts = []
        cast_emitted = set()

        def cast(name, b, dst, src):
            pending_casts.append((name, b, dst, src))

        def emit_cast():
            name, b, dst, src = pending_casts.pop(0)
            k = cast_i[0] % 8
            cast_i[0] += 1
            cast_emitted.add((name, b))
            S.add("pool", lambda e: e.dma_start(out=dst, in_=src), writes=[("scr_" + name, b)], dma=f"cast{k}")

        def drip_casts(n):
            for _ in range(n):
                if pending_casts:
                    emit_cast()

        def need_cast(name, b):
            while (name, b) not in cast_emitted:
                emit_cast()

        def cast_cols(name, b, w, c0, ncols=256, kc=KC):
            cast(name, b, scr[name][b].rearrange("p (k n) -> p k n", n=ncols), w[:, c0:c0 + ncols].rearrange("(k p) n -> p k n", p=128))

        def cast_ffn(f):
            for b in range(22):
                cast_cols(f + "g", b, Wd[f + "g"], b * 256)
                cast_cols(f + "u", b, Wd[f + "u"], b * 256)
            for oc in range(16):
                cast(f + "d", oc, scr[f + "d"][oc].rearrange("p (k n) -> p k n", n=128),
                     Wd[f + "d"][:, oc * 128:(oc + 1) * 128].rearrange("(k p) n -> p k n", p=128))

        WIN_BLOCKS = [OFF_QA, OFF_QA + 256, OFF_QA + 512, OFF_QA + 768, OFF_K, OFF_V]
        for half in range(2):
            for off in (OFF_IH, OFF_FH, OFF_QH, OFF_GH):
                WIN_BLOCKS += [off + half * 512, off + half * 512 + 256]
        if "ffn1" in stages:
            cast_ffn("w1")
        if "mixer" in stages:
            pre_first = [6, 7, 8, 9, 14, 15, 16, 17, 4, 5]
            for b in pre_first + [b for b in range(22) if b not in pre_first]:
                cast_cols("win", b, Wd["win"], WIN_BLOCKS[b])
            for b in range(8):
                cast_cols("wout", b, Wd["wout"], b * 256)
        if "ffn2" in stages:
            cast_ffn("w2")
        if "ple" in stages:
            for b in range(8):
                cast_cols("wpg", b, Wd["wpg"], b * 256)
            cast("wpp", 0, scr["wpp"][0].rearrange("p (k n) -> p k n", n=D), Wd["wpp"].rearrange("(k p) n -> p k n", p=128))

        slot_ctr = [0]

        def load_w(name, b, nelem, off=0):
            need_cast(name, b)
            drip_casts(2)
            s = slot_ctr[0] % NSLOT
            slot_ctr[0] += 1
            S.add("sp", lambda e: e.dma_start(out=slots[s][:, 0:nelem], in_=scr[name][b][:, off:off + nelem]),
                  reads=[("scr_" + name, b)], writes=[("slot", s)], dma=f"w{s}")
            return s

        def wview(s, n):
            return slots[s][:, :].rearrange("p (k n) -> p k n", n=n)

        def proj_fm(s, m, outbank, rhs_of_kc, rkeys, ncols=256, kc_n=KC, N=T):
            wv = wview(s, ncols)

            def fn(e):
                ins = None
                for kc in range(kc_n):
                    ins = e.matmul(ps[outbank][:, 0:N], lhsT=wv[:, kc, m * 128:(m + 1) * 128], rhs=rhs_of_kc(kc),
                                   start=(kc == 0), stop=(kc == kc_n - 1))
                return ins
            S.add("pe", fn, reads=[("slot", s)] + rkeys, writes=[("ps", outbank)])

        def norm_sums_group(gi, src_ap4, srckeys, first, last):
            i2 = gi % 2
            S.add("act", lambda e: e.activation(out=sqb[i2][:], in_=src_ap4, func=AF.Square), reads=srckeys, writes=[("sqb", i2)])

            def fn(e):
                ins = None
                for j in range(4):
                    ins = e.matmul(ps[7][:, 0:T], lhsT=ones_b[:], rhs=sqb[i2][:, j, :], start=(first and j == 0), stop=(last and j == 3))
                return ins
            S.add("pe", fn, reads=[("sqb", i2), ("ones_b", None)], writes=[("ps", 7)])

        def norm_sums_chunk(c, src_t, name):
            if c % 4 == 3:
                g = c // 4
                norm_sums_group(g, src_t[:, 4 * g:4 * g + 4, :], [(name, cc) for cc in range(4 * g, 4 * g + 4)], g == 0, g == 3)

        def rstd_from_ps7(n_feat):
            S.add("act", lambda e: e.activation(out=rstd[:], in_=ps[7][:, 0:T], func=AF.Ln, scale=1.0 / n_feat, bias=epsb[:, 0:1]),
                  reads=[("ps", 7), ("epsb", None)], writes=[("rstd", None)])
            S.add("act", lambda e: e.activation(out=rstd[:], in_=rstd[:], func=AF.Exp, scale=-0.5), reads=[("rstd", None)], writes=[("rstd", None)])

        epsb = sb("epsb", [128, 1])
        tmph = sb("tmph", [128, T])
        S.add("pool", lambda e: e.memset(epsb[:], EPS), writes=[("epsb", None)])

        def pre_norm(gcol):
            for c in range(KC):
                norm_sums_chunk(c, xT, "xT")
            rstd_from_ps7(D)
            for c in range(KC):
                if c % 3 == 2:
                    S.add("pool", lambda e, c=c: e.tensor_scalar(out=tmph[:], in0=xT[:, c, :], scalar1=gains[:, gcol + c:gcol + c + 1], scalar2=None, op0=ALU.mult),
                          reads=[("xT", c), ("gains", None)], writes=[("tmph", None)])
                    S.add("pool", lambda e, c=c: e.tensor_tensor(out=hT[:, c, :], in0=tmph[:], in1=rstd[:], op=ALU.mult),
                          reads=[("tmph", None), ("rstd", None)], writes=[("hT", c)])
                    continue
                S.add("dve", lambda e, c=c: e.scalar_tensor_tensor(out=hT[:, c, :], in0=xT[:, c, :], scalar=gains[:, gcol + c:gcol + c + 1],
                                                                    in1=rstd[:], op0=ALU.mult, op1=ALU.mult),
                      reads=[("xT", c), ("rstd", None), ("gains", None)], writes=[("hT", c)])

        def post_residual(gtile, gcol):
            rstd_from_ps7(D)
            for c in range(KC):
                S.add("dve", lambda e, c=c: e.scalar_tensor_tensor(out=yT[:, c, :], in0=yT[:, c, :], scalar=gtile[:, gcol + c:gcol + c + 1],
                                                                    in1=rstd[:], op0=ALU.mult, op1=ALU.mult),
                      reads=[("yT", c), ("rstd", None), ("gains", None), ("ghalf", None)], writes=[("yT", c)])
                S.add("pool", lambda e, c=c: e.tensor_tensor(out=xT[:, c, :], in0=xT[:, c, :], in1=yT[:, c, :], op=ALU.add),
                      reads=[("yT", c), ("xT", c)], writes=[("xT", c)])

        actT = big[:, :].rearrange("p (c t) -> p c t", t=T)

        def ffn(f, gpre, ghalf_col):
            pre_norm(gpre)
            for b in range(22):
                sg_ = load_w(f + "g", b, KC * 256)
                su_ = load_w(f + "u", b, KC * 256)
                for m in range(2):
                    ci = b * 2 + m
                    i2 = ci % 2
                    proj_fm(sg_, m, i2, lambda kc: hT[:, kc, :], [("hT", None)])
                    proj_fm(su_, m, 2 + i2, lambda kc: hT[:, kc, :], [("hT", None)])
                    S.add("act", lambda e, i2=i2: e.activation(out=tmps[i2][:], in_=ps[i2][:, 0:T], func=AF.Silu),
                          reads=[("ps", i2)], writes=[("tmps", i2)])
                    S.add("dve", lambda e, i2=i2, ci=ci: e.tensor_tensor(out=actT[:, ci, :], in0=tmps[i2][:], in1=ps[2 + i2][:, 0:T], op=ALU.mult),
                          reads=[("tmps", i2), ("ps", 2 + i2)], writes=[("big", ci)])
            pend = None
            for oc in range(16):
                HK = FC // 2
                sda = load_w(f + "d", oc, HK * 128, off=0)
                sdb = load_w(f + "d", oc, HK * 128, off=HK * 128)
                bank = 4 + oc % 2

                def fdn(e, sda=sda, sdb=sdb, bank=bank):
                    ins = None
                    for kc in range(FC):
                        wv = wview(sda if kc < HK else sdb, 128)
                        ins = e.matmul(ps[bank][:, 0:T], lhsT=wv[:, kc % HK, :], rhs=actT[:, kc, :], start=(kc == 0), stop=(kc == FC - 1))
                    return ins
                S.add("pe", fdn, reads=[("slot", sda), ("slot", sdb), ("big", None)], writes=[("ps", bank)])
                S.add("dve", lambda e, oc=oc, bank=bank: e.tensor_copy(out=yT[:, oc, :], in_=ps[bank][:, 0:T]),
                      reads=[("ps", bank)], writes=[("yT", oc)])
                if pend is not None:
                    pend()
                pend = (lambda oc=oc: norm_sums_chunk(oc, yT, "yT"))
            pend()
            post_residual(ghalf, ghalf_col)

        def load_tile(xsrc, psrc, t0):
            S.add("sp", lambda e: e.dma_start(out=stg[:], in_=xsrc[t0:t0 + T, :].rearrange("(b p) d -> p b d", p=128)),
                  writes=[("stg", None)], dma="xin")
            if psrc is not None:
                S.add("sp", lambda e: e.dma_start(out=pstg[:], in_=psrc[t0:t0 + T, :].rearrange("(b p) d -> p b d", p=128)),
                      writes=[("pstg", None)], dma="pin")
            for c in range(KC):
                bank = 5 + c % 2

                def fn(e, c=c, bank=bank):
                    ins = None
                    for b in range(NB):
                        ins = e.transpose(ps[bank][:, b * 128:(b + 1) * 128], stg[:, b, c * 128:(c + 1) * 128], ident[:])
                    return ins
                S.add("pe", fn, reads=[("stg", None), ("ident", None)], writes=[("ps", bank)])
                eng = "act" if c % 2 == 0 else "dve"
                if eng == "act":
                    S.add("act", lambda e, c=c, bank=bank: e.activation(out=xT[:, c, :], in_=ps[bank][:, 0:T], func=AF.Copy),
                          reads=[("ps", bank)], writes=[("xT", c)])
                else:
                    S.add("dve", lambda e, c=c, bank=bank: e.tensor_copy(out=xT[:, c, :], in_=ps[bank][:, 0:T]),
                          reads=[("ps", bank)], writes=[("xT", c)])
            for c in range(2 if psrc is not None else 0):
                bank = 5 + c % 2

                def fn(e, c=c, bank=bank):
                    ins = None
                    for b in range(NB):
                        ins = e.transpose(ps[bank][:, b * 128:(b + 1) * 128], pstg[:, b, c * 128:(c + 1) * 128], ident[:])
                    return ins
                S.add("pe", fn, reads=[("pstg", None), ("ident", None)], writes=[("ps", bank)])
                S.add("act", lambda e, c=c, bank=bank: e.activation(out=pT[:, c, :], in_=ps[bank][:, 0:T], func=AF.Copy),
                      reads=[("ps", bank)], writes=[("pT", c)])

        out_ops = []

        def store_tile(ydst, t0):
            for c in range(KC):
                bank = 5 + c % 2

                def fn(e, c=c, bank=bank):
                    ins = None
                    for b in range(NB):
                        ins = e.transpose(ps[bank][:, b * 128:(b + 1) * 128], xT[:, c, b * 128:(b + 1) * 128], ident[:])
                    return ins
                S.add("pe", fn, reads=[("xT", c), ("ident", None)], writes=[("ps", bank)])
                if c % 2 == 0:
                    S.add("act", lambda e, c=c, bank=bank: e.activation(out=stg[:, :, c * 128:(c + 1) * 128],
                                                                        in_=ps[bank][:, 0:T].rearrange("p (b n) -> p b n", n=128), func=AF.Copy),
                          reads=[("ps", bank)], writes=[("stg", c)])
                else:
                    S.add("dve", lambda e, c=c, bank=bank: e.tensor_copy(out=stg[:, :, c * 128:(c + 1) * 128],
                                                                         in_=ps[bank][:, 0:T].rearrange("p (b n) -> p b n", n=128)),
                          reads=[("ps", bank)], writes=[("stg", c)])
            o = S.add("sp", lambda e: e.dma_start(out=ydst[t0:t0 + T, :].rearrange("(b p) d -> p b d", p=128), in_=stg[:]),
                      reads=[("stg", None)], dma="yout")
            out_ops.append(o)

        def attention(pieces_of_chunk):
            for ch in range(NCH):
                pieces = pieces_of_chunk(ch)
                par = ch % 2
                for pi, pc in enumerate(pieces):
                    bank = pi + 2 * par
                    M = pc["M"]

                    def fn(e, pc=pc, bank=bank, M=M, ch=ch):
                        ins = None
                        for kv in range(2):
                            ins = e.matmul(ps[bank][0:M, kv * 256:(kv + 1) * 256], lhsT=pc["kt"](kv),
                                           rhs=qT[:, 4 * kv:4 * kv + 4, ch * 64:(ch + 1) * 64], start=True, stop=True)
                        return ins
                    S.add("pe", fn, reads=[("qT", None)] + pc["kkeys"], writes=[("ps", bank)])
                    pt = PT[bank]
                    bi = pc["bias"]
                    S.add("dve", lambda e, bank=bank, M=M, bi=bi, pi=pi: e.tensor_tensor(out=sc[pi][0:M, :], in0=ps[bank][0:M, :], in1=biast[0:M, bi, :], op=ALU.add),
                          reads=[("ps", bank), ("biast", None)], writes=[("sc", pi)])
                    S.add("act", lambda e, pt=pt, M=M, pi=pi: e.activation(out=pt[0:M, :], in_=sc[pi][0:M, :], func=AF.Exp),
                          reads=[("sc", pi)], writes=[("PT", bank)])
                    if pc.get("mask"):
                        S.add("pool", lambda e, pt=pt, M=M: e.tensor_scalar(out=pt[0:M, :], in0=pt[0:M, :], scalar1=flagt[0:M, 0:1], scalar2=None, op0=ALU.mult),
                              reads=[("PT", bank), ("flagt", None)], writes=[("PT", bank)])
                ob, db = 4 + par, 6 + par

                def fo(e, pieces=pieces, ob=ob, par=par):
                    ins = None
                    for kv in range(2):
                        for pi, pc in enumerate(pieces):
                            p0, p1 = pc["pr"]
                            ins = e.matmul(ps[ob][:, kv * 256:(kv + 1) * 256], lhsT=pc["v"](kv), rhs=PT[pi + 2 * par][p0:p1, kv * 256:(kv + 1) * 256],
                                           start=(pi == 0), stop=(pi == len(pieces) - 1))
                    return ins

                def fd(e, pieces=pieces, db=db, par=par):
                    ins = None
                    for kv in range(2):
                        for pi, pc in enumerate(pieces):
                            p0, p1 = pc["pr"]
                            ins = e.matmul(ps[db][:, kv * 256:(kv + 1) * 256], lhsT=ones_b[p0:p1, :], rhs=PT[pi + 2 * par][p0:p1, kv * 256:(kv + 1) * 256],
                                           start=(pi == 0), stop=(pi == len(pieces) - 1))
                    return ins
                rk = [("PT", pi + 2 * par) for pi in range(len(pieces))]
                vk = []
                for pc in pieces:
                    vk += pc["vkeys"]
                S.add("pe", fo, reads=rk + vk, writes=[("ps", ob)])
                S.add("pe", fd, reads=rk + [("ones_b", None)], writes=[("ps", db)])
                S.add("dve", lambda e, db=db: e.tensor_tensor(out=densb[:, :].rearrange("p (h q) -> p h q", q=64),
                                                              in0=ps[db][:, :].rearrange("p (h q) -> p h q", q=64),
                                                              in1=esink[:, :].unsqueeze(2).broadcast_to([128, 8, 64]), op=ALU.add),
                      reads=[("ps", db), ("esink", None)], writes=[("densb", None)])
                S.add("dve", lambda e: e.reciprocal(out=densb[:], in_=densb[:]), reads=[("densb", None)], writes=[("densb", None)])
                S.add("dve", lambda e, ob=ob, ch=ch: e.tensor_tensor(out=mixT[:, 0:8, ch * 64:(ch + 1) * 64],
                                                                     in0=ps[ob][:, :].rearrange("p (h q) -> p h q", q=64),
                                                                     in1=densb[:, :].rearrange("p (h q) -> p h q", q=64), op=ALU.mult),
                      reads=[("ps", ob), ("densb", None)], writes=[("mixT", h) for h in range(8)])

        def hgrn_half(half, seqs, state_only=False):
            hh = half * 4
            A3 = hA.rearrange("p (m t) -> p m t", t=T); B3 = hB.rearrange("p (m t) -> p m t", t=T)
            C3 = hC.rearrange("p (m t) -> p m t", t=T); Q3 = hQ.rearrange("p (m t) -> p m t", t=T)
            G3 = hG.rearrange("p (m t) -> p m t", t=T)
            kA, kB, kC, kQ, kG = [hkeys(i) for i in range(5)]
            base = 6 + half * 8
            s0 = load_w("win", base + 0, KC * 256)
            s1 = load_w("win", base + 1, KC * 256)
            for blk in range(NB):
                for j, s_ in enumerate((s0, s1)):
                    bank = blk * 2 + j

                    def fn(e, blk=blk, bank=bank, s_=s_):
                        ins = None
                        wv = wview(s_, 256)
                        for kc in range(KC):
                            ins = e.matmul(ps[bank][:, 0:256], lhsT=hT[:, kc, blk * 128:(blk + 1) * 128], rhs=wv[:, kc, :],
                                           start=(kc == 0), stop=(kc == KC - 1))
                        return ins
                    S.add("pe", fn, reads=[("slot", s_), ("hT", None)], writes=[("ps", bank)])
                    S.add("act", lambda e, blk=blk, bank=bank, j=j: e.activation(out=itok[:, blk, j * 256:(j + 1) * 256], in_=ps[bank][:, 0:256], func=AF.Copy),
                          reads=[("ps", bank)], writes=[("itok", blk)])
            for gi, (dst3, keys, func) in enumerate(((A3, kA, AF.Sigmoid), (Q3, kQ, AF.Copy), (G3, kG, AF.Silu))):
                if state_only and gi > 0:
                    continue
                for j in range(2):
                    s = load_w("win", base + 2 + gi * 2 + j, KC * 256)
                    for m in range(2):
                        mm = j * 2 + m
                        bank = 4 + (mm % 2) + 2 * (gi % 2)
                        proj_fm(s, m, bank, lambda kc: hT[:, kc, :], [("hT", None)])
                        S.add("act", lambda e, dst3=dst3, mm=mm, bank=bank, func=func: e.activation(out=dst3[:, mm, :], in_=ps[bank][:, 0:T], func=func),
                              reads=[("ps", bank)], writes=keys[mm * 2:mm * 2 + 2])
            for m in range(4):
                S.add("dve", lambda e, m=m: e.tensor_scalar(out=B3[:, m, :], in0=A3[:, m, :], scalar1=oml[:, hh + m:hh + m + 1], scalar2=lbt[:, hh + m:hh + m + 1],
                                                            op0=ALU.mult, op1=ALU.add),
                      reads=kA[m * 2:m * 2 + 2] + [("oml", None), ("lbt", None)], writes=kB[m * 2:m * 2 + 2])
                S.add("pool", lambda e, m=m: e.tensor_scalar(out=C3[:, m, :], in0=A3[:, m, :], scalar1=noml[:, hh + m:hh + m + 1], scalar2=oml[:, hh + m:hh + m + 1],
                                                             op0=ALU.mult, op1=ALU.add),
                      reads=kA[m * 2:m * 2 + 2] + [("oml", None), ("noml", None)], writes=kC[m * 2:m * 2 + 2])
            S.add("act", lambda e: e.activation(out=hA, in_=hB, func=AF.Ln), reads=kB, writes=kA)
            S.add("dve", lambda e: e.tensor_tensor_scan(out=hB, data0=smask[:, :], data1=hA, initial=0.0, op0=ALU.mult, op1=ALU.add),
                  reads=kA + [("smask", None)], writes=kB)
            if not state_only:
                S.add("act", lambda e: e.activation(out=hA, in_=hB, func=AF.Exp), reads=kB, writes=kA)
                S.add("dve", lambda e: e.tensor_tensor(out=hQ, in0=hQ, in1=hA, op=ALU.mult), reads=kQ + kA, writes=kQ)
            S.add("act", lambda e: e.activation(out=Dd[:, :, :], in_=hB.rearrange("p (m n l) -> p m n l", m=4, l=16)[:, :, :, 15], func=AF.Exp),
                  reads=kB, writes=[("Dd", None)])
            S.add("act", lambda e: e.activation(out=hA, in_=hB, func=AF.Exp, scale=-1.0), reads=kB + kQ, writes=kA)
            S.add("dve", lambda e: e.tensor_tensor(out=hC, in0=hC, in1=hA, op=ALU.mult), reads=kC + kA, writes=kC)
            S.add("pool", lambda e: e.tensor_tensor(out=hA.rearrange("p (m n l) -> p m n l", m=4, l=16), in0=hC.rearrange("p (m n l) -> p m n l", m=4, l=16),
                                                    in1=Dd[:, :, :].unsqueeze(3).broadcast_to([128, 4, NBLK, 16]), op=ALU.mult),
                  reads=kC + [("Dd", None)], writes=kA)
            seq_at = {s[0]: s for s in seqs}
            seq_end = {s[0] + s[1] - 1: s for s in seqs}
            cur = 0
            for g in range(NB):
                gs = slice(g * 128, (g + 1) * 128)

                def ft(e, gs=gs):
                    ins = None
                    for m in range(4):
                        ins = e.transpose(ps[0][:, m * 128:(m + 1) * 128], A3[:, m, gs], ident[:])
                    return ins
                S.add("pe", ft, reads=kA + [("ident", None)], writes=[("ps", 0)])
                S.add("act", lambda e: e.activation(out=kend_tok[:], in_=ps[0][:, :], func=AF.Copy), reads=[("ps", 0)], writes=[("kend_tok", None)])

                def fa(e, gs=gs):
                    ins = None
                    for m in range(4):
                        ins = e.matmul(ps[1][:, m * 128:(m + 1) * 128], lhsT=C3[:, m, gs], rhs=Q3[:, m, gs], start=True, stop=True)
                    return ins
                if not state_only:
                    S.add("pe", fa, reads=kC + kQ, writes=[("ps", 1)])
                    S.add("dve", lambda e: e.tensor_tensor(out=Am[:, :].rearrange("p (m t) -> p m t", t=128), in0=ps[1][:, :].rearrange("p (m t) -> p m t", t=128),
                                                           in1=bmask[:, :].unsqueeze(1).broadcast_to([128, 4, 128]), op=ALU.mult),
                          reads=[("ps", 1), ("bmask", None)], writes=[("Am", None)])
                S.add("pool", lambda e, g=g: e.tensor_tensor(out=im4[:], in0=itok[:, g, :].unsqueeze(1).broadcast_to([128, 4, 512]),
                                                             in1=imask[:, :].unsqueeze(2).broadcast_to([128, 4, 512]), op=ALU.mult),
                      reads=[("itok", g), ("imask", None)], writes=[("im4", None)])

                def fi(e, g=g):
                    ins = None
                    for m in range(4):
                        ins = e.matmul(ps[2][:, m * 128:(m + 1) * 128], lhsT=itok[:, g, m * 128:(m + 1) * 128], rhs=Am[:, m * 128:(m + 1) * 128],
                                       start=(m == 0), stop=False)
                    return ins
                if not state_only:
                    S.add("pe", fi, reads=[("itok", g), ("Am", None)], writes=[("ps", 2)])
                for w in range(2):
                    for m in range(4):
                        S.add("pe", lambda e, w=w, m=m: e.matmul(ps[4 + m][:, :], lhsT=kend_tok[64 * w:64 * w + 64, m * 128:(m + 1) * 128],
                                                                 rhs=im4[64 * w:64 * w + 64, :, m * 128:(m + 1) * 128], start=True, stop=True),
                              reads=[("kend_tok", None), ("im4", None)], writes=[("ps", 4 + m)])
                    for q in range(4):
                        n = g * 8 + w * 4 + q
                        if n in seq_at:
                            init = seq_at[n][2]
                            if init[0] == "zero":
                                S.add("pool", lambda e, cur=cur: e.memset(Sring[cur][:], 0.0), writes=[("Sring", cur)])
                            elif init[0] == "carry":
                                S.add("pool", lambda e, cur=cur: e.tensor_copy(out=Sring[cur][:], in_=Scarry[half][:]),
                                      reads=[("Scarry", half)], writes=[("Sring", cur)])
                            else:
                                S.add("sp", lambda e, cur=cur, ap=init[1]: e.dma_start(out=Sring[cur][:], in_=ap), writes=[("Sring", cur)], dma="stin")
                        def fin_(e, cur=cur, n=n, g=g):
                            ins = None
                            col = (n - g * 8) * 16
                            for m in range(4):
                                ins = e.matmul(ps[2][:, m * 128 + col:m * 128 + col + 16], lhsT=Sring[cur][:, m, :], rhs=Q3[:, m, n * 16:(n + 1) * 16],
                                               start=False, stop=(n % 8 == 7 and m == 3))
                            return ins
                        if not state_only:
                            S.add("pe", fin_, reads=[("Sring", cur)] + kQ, writes=[("ps", 2)])
                        nxt = (cur + 1) % 4
                        for m in range(4):
                            S.add("dve", lambda e, m=m, cur=cur, nxt=nxt, n=n, q=q: e.scalar_tensor_tensor(
                                out=Sring[nxt][:, m, :], in0=Sring[cur][:, m, :], scalar=Dd[:, m, n:n + 1], in1=ps[4 + m][:, q * 128:(q + 1) * 128],
                                op0=ALU.mult, op1=ALU.add),
                                reads=[("Sring", cur), ("Dd", None), ("ps", 4 + m)], writes=[("Sring", nxt)])
                        cur = nxt
                        if n in seq_end:
                            fin = seq_end[n][3]
                            if fin is not None:
                                if fin[0] == "carry":
                                    S.add("pool", lambda e, cur=cur: e.tensor_copy(out=Scarry[half][:], in_=Sring[cur][:]),
                                          reads=[("Sring", cur)], writes=[("Scarry", half)])
                                else:
                                    o = S.add("sp", lambda e, cur=cur, ap=fin[1]: e.dma_start(out=ap, in_=Sring[cur][:]), reads=[("Sring", cur)], dma="stout")
                                    out_ops.append(o)
                if state_only:
                    continue
                S.add("act", lambda e: e.activation(out=sq32[:], in_=ps[2][:, :], func=AF.Square), reads=[("ps", 2)], writes=[("sq32", None)])
                S.add("pe", lambda e: e.matmul(ps[3][:, :], lhsT=ones_f[:], rhs=sq32[:], start=True, stop=True),
                      reads=[("sq32", None), ("ones_f", None)], writes=[("ps", 3)])
                S.add("act", lambda e: e.activation(out=rs32[:], in_=ps[3][:, :], func=AF.Ln, scale=1.0 / 128, bias=epsb[:, 0:1]),
                      reads=[("ps", 3), ("epsb", None)], writes=[("rs32", None)])
                S.add("act", lambda e: e.activation(out=rs32[:], in_=rs32[:], func=AF.Exp, scale=-0.5), reads=[("rs32", None)], writes=[("rs32", None)])
                S.add("dve", lambda e: e.tensor_tensor(out=tmpo[:], in0=ps[2][:, :], in1=rs32[:], op=ALU.mult),
                      reads=[("ps", 2), ("rs32", None)], writes=[("tmpo", None)])
                S.add("dve", lambda e, gs=gs: e.scalar_tensor_tensor(out=mixT[:, 8 + hh:8 + hh + 4, gs], in0=tmpo[:, :].rearrange("p (m t) -> p m t", t=128),
                                                                     scalar=gains[:, GC_HN:GC_HN + 1], in1=G3[:, :, gs], op0=ALU.mult, op1=ALU.mult),
                      reads=[("tmpo", None), ("gains", None)] + kG, writes=[("mixT", 8 + hh + m) for m in range(4)])

        def mixer(tile):
            pre_norm(GC_MPRE)
            scale = 1.0 / math.sqrt(128.0)
            for b in range(4):
                s = load_w("win", b, KC * 256)
                for m in range(2):
                    h = b * 2 + m
                    bank = h % 2
                    proj_fm(s, m, bank, lambda kc: hT[:, kc, :], [("hT", None)])
                    S.add("act", lambda e, h=h, bank=bank: e.activation(out=qT[:, h, :], in_=ps[bank][:, 0:T], func=AF.Copy, scale=scale),
                          reads=[("ps", bank)], writes=[("qT", h)])
            if "q_only" in MIX_PARTS:
                return
            if "E5" in MIX_PARTS:
                slot_ctr[0] += 1
            sk = load_w("win", 4, KC * 256)
            sv = load_w("win", 5, KC * 256)
            for m in range(2):
                bank = 2 + m
                proj_fm(sk, m, bank, lambda kc: hT[:, kc, :], [("hT", None)])
                S.add("dve", lambda e, m=m, bank=bank: e.tensor_copy(out=KT[:, m, 128:128 + T], in_=ps[bank][:, 0:T]),
                      reads=[("ps", bank)], writes=[("KT", 1), ("KT", 2)])
            if "kT_only" in MIX_PARTS:
                return
            for blk in range(NB):
                for j, s_ in enumerate((sk, sv)):
                    bank = (blk * 2 + j) if "E7" in MIX_PARTS else (4 + blk * 2 + j)
                    if ("E1" in MIX_PARTS and j == 1) or ("E3" in MIX_PARTS and j == 0):
                        continue

                    def fn(e, blk=blk, bank=bank, s_=s_):
                        ins = None
                        wv = wview(sk if "E4" in MIX_PARTS else s_, 256)
                        for kc in range(KC):
                            ins = e.matmul(ps[bank][:, 0:256], lhsT=hT[:, kc, blk * 128:(blk + 1) * 128], rhs=wv[:, kc, :],
                                           start=(kc == 0), stop=(kc == KC - 1))
                        return ins
                    S.add("pe", fn, reads=[("slot", sk if "E6" in MIX_PARTS else s_), ("hT", None)], writes=[("ps", bank)])
                    if "kvtok_noevac" in MIX_PARTS:
                        continue
                    if j == 1:
                        S.add("act", lambda e, blk=blk, bank=bank: e.activation(out=Vt[:, 1 + blk, :], in_=ps[bank][:, 0:256], func=AF.Copy),
                              reads=[("ps", bank)], writes=[("Vt", 1 + blk)])
                    if tile["kvout"]:
                        S.add("dve", lambda e, blk=blk, bank=bank, j=j: e.tensor_copy(out=kvstg[:, blk, j * 256:(j + 1) * 256], in_=ps[bank][:, 0:256]),
                              reads=[("ps", bank)], writes=[("kvstg", blk)])
            if "no_kvdma" not in MIX_PARTS:
                tile["kv_emit"]()
            if "kvtok_only" in MIX_PARTS:
                return
            if "attn" in MIX_PARTS:
                attention(tile["pieces"])
            tile["post_attn"]()
            for half in range(2):
                if "hgrn" in MIX_PARTS:
                    hgrn_half(half, tile["seqs"](half))
            if DEBUG_MIX and tile.get("ti") is not None:
                o = S.add("sp", lambda e: e.dma_start(out=dbgmix[tile["ti"]], in_=mixT[:, :, :].rearrange("p c t -> p (c t)")),
                          reads=[("mixT", None)], dma="dbg")
                out_ops.append(o)
            for b in range(8 if "wout" in MIX_PARTS else 0):
                s = load_w("wout", b, KC * 256)
                for m in range(2):
                    oc = b * 2 + m
                    bank = 4 + oc % 2
                    proj_fm(s, m, bank, lambda kc: mixT[:, kc, :], [("mixT", None)])
                    S.add("dve", lambda e, oc=oc, bank=bank: e.tensor_copy(out=yT[:, oc, :], in_=ps[bank][:, 0:T]),
                          reads=[("ps", bank)], writes=[("yT", oc)])
                    norm_sums_chunk(oc, yT, "yT")
            if "wout" in MIX_PARTS:
                post_residual(gains, GC_MPOST)

        def ple():
            pre_norm(GC_PPRE)
            need_cast("wpp", 0)
            wkeys = [("big", c) for c in range(2 * D // T)]
            S.add("sp", lambda e: e.dma_start(out=big[:, 0:2 * D], in_=scr["wpp"][0][:, 0:2 * D]), reads=[("scr_wpp", 0)], writes=wkeys, dma="wpp")
            wpv = big[:, 0:2 * D].rearrange("p (k n) -> p k n", n=D)
            for b in range(8):
                s = load_w("wpg", b, KC * 256)
                for m in range(2):
                    oc = b * 2 + m
                    i2 = oc % 2
                    proj_fm(s, m, i2, lambda kc: hT[:, kc, :], [("hT", None)])

                    def fp(e, oc=oc, i2=i2):
                        ins = None
                        for kc in range(2):
                            ins = e.matmul(ps[2 + i2][:, 0:T], lhsT=wpv[:, kc, oc * 128:(oc + 1) * 128], rhs=pT[:, kc, :], start=(kc == 0), stop=(kc == 1))
                        return ins
                    S.add("pe", fp, reads=wkeys + [("pT", None)], writes=[("ps", 2 + i2)])
                    S.add("act", lambda e, i2=i2: e.activation(out=tmps[i2][:], in_=ps[i2][:, 0:T], func=AF.Sigmoid),
                          reads=[("ps", i2)], writes=[("tmps", i2)])
                    S.add("dve", lambda e, i2=i2, oc=oc: e.tensor_tensor(out=yT[:, oc, :], in0=tmps[i2][:], in1=ps[2 + i2][:, 0:T], op=ALU.mult),
                          reads=[("tmps", i2), ("ps", 2 + i2)], writes=[("yT", oc)])
                    norm_sums_chunk(oc, yT, "yT")
            post_residual(gains, GC_PPOST)

        def prompt_tile(ti):
            first = (ti == 0) and first_tile_is_seq_start and NPRE == 0
            last = (ti == NPT - 1)

            def pieces(ch):
                j = ch // 2
                pcs = []
                if ch % 2 == 0:
                    if not (first and j == 0):
                        pcs.append(dict(M=128, kt=lambda kv, j=j: KT[:, kv, j * 128:(j + 1) * 128], kkeys=[("KT", j)],
                                        v=lambda kv, j=j: Vt[:, j, kv * 128:(kv + 1) * 128], vkeys=[("Vt", j)], pr=(0, 128), bias=0,
                                        mask=(NPRE > 0 and ti == 0 and j == 0)))
                    pcs.append(dict(M=64, kt=lambda kv, j=j: KT[:, kv, (j + 1) * 128:(j + 1) * 128 + 64], kkeys=[("KT", j + 1)],
                                    v=lambda kv, j=j: Vt[0:64, j + 1, kv * 128:(kv + 1) * 128], vkeys=[("Vt", j + 1)], pr=(0, 64), bias=1))
                else:
                    if not (first and j == 0):
                        pcs.append(dict(M=128, kt=lambda kv, j=j: KT[:, kv, j * 128:(j + 1) * 128], kkeys=[("KT", j)],
                                        v=lambda kv, j=j: Vt[64:128, j, kv * 128:(kv + 1) * 128], vkeys=[("Vt", j)], pr=(64, 128), bias=2,
                                        mask=(NPRE > 0 and ti == 0 and j == 0)))
                    pcs.append(dict(M=128, kt=lambda kv, j=j: KT[:, kv, (j + 1) * 128:(j + 2) * 128], kkeys=[("KT", j + 1)],
                                    v=lambda kv, j=j: Vt[:, j + 1, kv * 128:(kv + 1) * 128], vkeys=[("Vt", j + 1)], pr=(0, 128), bias=3))
                return pcs

            def kv_emit():
                if last:
                    o = S.add("sp", lambda e: e.dma_start(out=kp[:, :], in_=kvstg[:, NB - 1, 0:256]), reads=[("kvstg", NB - 1)], dma="kvout")
                    out_ops.append(o)
                    o = S.add("sp", lambda e: e.dma_start(out=vp[:, :], in_=kvstg[:, NB - 1, 256:512]), reads=[("kvstg", NB - 1)], dma="kvout")
                    out_ops.append(o)

            def post_attn():
                if not last:
                    S.add("pool", lambda e: e.tensor_copy(out=KT[:, :, 0:128], in_=KT[:, :, NB * 128:(NB + 1) * 128]),
                          reads=[("KT", NB)], writes=[("KT", 0)])
                    S.add("pool", lambda e: e.tensor_copy(out=Vt[:, 0, :], in_=Vt[:, NB, :]), reads=[("Vt", NB)], writes=[("Vt", 0)])

            def seqs(half):
                init = ("zero",) if first else ("carry",)
                fin = ("dma", spo[half * 4:half * 4 + 4].rearrange("h k v -> k h v")) if last else ("carry",)
                return [(0, NBLK, init, fin)]

            load_tile(xp, pp, ti * T)
            if "ffn1" in stages:
                ffn("w1", GC_F1PRE, 0)
            if "mixer" in stages:
                mixer(dict(kvout=last, kv_emit=kv_emit, pieces=pieces, post_attn=post_attn, seqs=seqs, ti=ti))
            if "ffn2" in stages:
                ffn("w2", GC_F2PRE, 16)
            if "ple" in stages:
                ple()
            store_tile(yp, ti * T)

        def sample_tile():
            def pieces(ch):
                s = ch
                j, hf = s // 2, s % 2
                pcs = [dict(M=128, kt=lambda kv, s=s: KTc[:, kv, s * 128:(s + 1) * 128], kkeys=[("KTc", None)],
                            v=lambda kv, s=s: Vc[:, s, kv * 128:(kv + 1) * 128], vkeys=[("Vc", None)], pr=(0, 128), bias=0)]
                if hf == 0:
                    pcs.append(dict(M=64, kt=lambda kv, j=j: KT[:, kv, (j + 1) * 128:(j + 1) * 128 + 64], kkeys=[("KT", j + 1)],
                                    v=lambda kv, j=j: Vt[0:64, j + 1, kv * 128:(kv + 1) * 128], vkeys=[("Vt", j + 1)], pr=(0, 64), bias=1))
                else:
                    pcs.append(dict(M=128, kt=lambda kv, j=j: KT[:, kv, (j + 1) * 128:(j + 2) * 128], kkeys=[("KT", j + 1)],
                                    v=lambda kv, j=j: Vt[64:128, j + 1, kv * 128:(kv + 1) * 128], vkeys=[("Vt", j + 1)], pr=(64, 128), bias=4))
                return pcs

            def kv_emit():
                S.add("sp", lambda e: e.dma_start(out=stg[:, 0, 0:1024].rearrange("p (s n) -> p s n", n=256), in_=ck.rearrange("s p n -> p s n")),
                      writes=[("stg", None)], dma="xin")
                S.add("pool", lambda e: e.dma_start(out=Vc[:], in_=cv.rearrange("s p n -> p s n")), writes=[("Vc", None)], dma="cvin")
                for kv in range(2):
                    def fn(e, kv=kv):
                        ins = None
                        for s in range(4):
                            ins = e.transpose(ps[6][:, s * 128:(s + 1) * 128], stg[:, 0, s * 256 + kv * 128:s * 256 + (kv + 1) * 128], ident[:])
                        return ins
                    S.add("pe", fn, reads=[("stg", None), ("ident", None)], writes=[("ps", 6)])
                    S.add("act", lambda e, kv=kv: e.activation(out=KTc[:, kv, :], in_=ps[6][:, :], func=AF.Copy), reads=[("ps", 6)], writes=[("KTc", None)])
                for s in range(4):
                    j, hf = s // 2, s % 2
                    o = S.add("sp", lambda e, s=s, j=j, hf=hf: e.dma_start(out=kso[s, 64:128, :], in_=kvstg[hf * 64:hf * 64 + 64, j, 0:256]),
                              reads=[("kvstg", j)], dma="kvout")
                    out_ops.append(o)
                    o = S.add("sp", lambda e, s=s, j=j, hf=hf: e.dma_start(out=vso[s, 64:128, :], in_=kvstg[hf * 64:hf * 64 + 64, j, 256:512]),
                              reads=[("kvstg", j)], dma="kvout")
                    out_ops.append(o)
                o = S.add("sp", lambda e: e.dma_start(out=kso[:, 0:64, :], in_=ck[:, 64:128, :]), dma="kvcp")
                out_ops.append(o)
                o = S.add("sp", lambda e: e.dma_start(out=vso[:, 0:64, :], in_=cv[:, 64:128, :]), dma="kvcp")
                out_ops.append(o)

            def seqs(half):
                r = []
                for s in range(4):
                    r.append((s * 4, 4, ("dma", stin[s, half * 4:half * 4 + 4].rearrange("h k v -> k h v")),
                              ("dma", sso[s, half * 4:half * 4 + 4].rearrange("h k v -> k h v"))))
                return r

            load_tile(xs, pss, 0)
            if "ffn1" in stages:
                ffn("w1", GC_F1PRE, 0)
            if "mixer" in stages:
                mixer(dict(kvout=True, kv_emit=kv_emit, pieces=pieces, post_attn=lambda: None, seqs=seqs))
            if "ffn2" in stages:
                ffn("w2", GC_F2PRE, 16)
            if "ple" in stages:
                ple()
            store_tile(ys, 0)

        def pre_tile(ti):
            lastp = (ti == NPRE - 1)
            load_tile(xpre, None, ti * T)
            ffn("w1", GC_F1PRE, 0)
            pre_norm(GC_MPRE)
            if lastp:
                sk = load_w("win", 4, KC * 256)
                sv = load_w("win", 5, KC * 256)
                for m in range(2):
                    proj_fm(sk, m, 2 + m, lambda kc: hT[:, kc, 128:256], [("hT", None)], N=128)
                    S.add("dve", lambda e, m=m: e.tensor_copy(out=KT[:, m, 0:128], in_=ps[2 + m][:, 0:128]), reads=[("ps", 2 + m)], writes=[("KT", 0)])

                def fv(e):
                    ins = None
                    wv = wview(sv, 256)
                    for kc in range(KC):
                        ins = e.matmul(ps[4][:, 0:256], lhsT=hT[:, kc, 128:256], rhs=wv[:, kc, :], start=(kc == 0), stop=(kc == KC - 1))
                    return ins
                S.add("pe", fv, reads=[("slot", sv), ("hT", None)], writes=[("ps", 4)])
                S.add("act", lambda e: e.activation(out=Vt[:, 0, :], in_=ps[4][:, 0:256], func=AF.Copy), reads=[("ps", 4)], writes=[("Vt", 0)])
            for half in range(2):
                hgrn_half(half, [(0, NBLK, ("zero",) if ti == 0 else ("carry",), ("carry",))], state_only=True)
            if lastp:
                for half in range(2):
                    S.add("dve", lambda e, half=half: e.tensor_scalar(out=Scarry[half][:, :, :].rearrange("p m v -> p (m v)"),
                                                                     in0=Scarry[half][:, :, :].rearrange("p m v -> p (m v)"),
                                                                     scalar1=flagt[:, 0:1], scalar2=None, op0=ALU.mult),
                          reads=[("Scarry", half), ("flagt", None)], writes=[("Scarry", half)])

        drip_casts(16)
        if sample and NPRE == 0:
            sample_tile()
        for ti in range(NPRE):
            pre_tile(ti)
        for ti in range(NPT):
            prompt_tile(ti)
        if sample and NPRE > 0:
            sample_tile()
        assert not pending_casts or True
        S.add("sp", None, extra_deps=out_ops)
        S.emit(nc, es)
    return nc


def t5_bucket_np(rel):
    half, max_exact = 16, 8
    n = np.abs(rel)
    nf = np.maximum(n, 1).astype(np.float32)
    large = max_exact + (np.log(nf / np.float32(max_exact)) / np.float32(math.log(128 / max_exact)) * (half - max_exact)).astype(np.int32)
    large = np.minimum(large, half - 1)
    return np.where(rel > 0, half, 0) + np.where(n < max_exact, n, large)


def host_consts(rel_bias_table, gain_vecs, hgrn_norm, lb_logits, sinks):
    G = np.zeros((128, NGC), np.float32)
    for i, v in enumerate(gain_vecs):
        G[:, 16 * i:16 * i + 16] = np.asarray(v, np.float32).reshape(16, 128).T
    G[:, GC_HN] = np.asarray(hgrn_norm, np.float32).reshape(128)
    G[:, GC_L0:GC_L0 + 8] = np.asarray(lb_logits[0], np.float32).reshape(8, 128).T
    G[:, GC_L1:GC_L1 + 8] = np.asarray(lb_logits[1], np.float32).reshape(8, 128).T
    G[:, GC_SINK:GC_SINK + 8] = np.asarray(sinks, np.float32).reshape(1, 8)
    tab = np.asarray(rel_bias_table, np.float32)
    j = np.arange(192)[:, None]
    i = np.arange(64)[None, :]
    bk = t5_bucket_np(j - 128 - i)
    full = tab[bk]
    full = np.transpose(full, (0, 2, 1)).reshape(192, 512)
    bt = np.zeros((5, 128, 512), np.float32)
    bt[0, :, :] = full[0:128]
    bt[1, 0:64, :] = full[128:192]
    bt[2, 64:128, :] = full[0:64]
    bt[3, :, :] = full[64:192]
    bt[4, 64:128, :] = full[128:192]
    ident = np.eye(128, dtype=np.float32)
    s = np.arange(128)[:, None]
    t = np.arange(128)[None, :]
    bmask = ((s // 16 == t // 16) & (s <= t)).astype(np.float32)
    smask = np.ones((128, 4 * T), np.float32)
    smask[:, ::16] = 0.0
    imask = (((np.arange(128)[:, None] % 64) // 16) == np.arange(4)[None, :]).astype(np.float32)
    return dict(gains=G, biast=bt, ident=ident, bmask=bmask, smask=smask, imask=imask)


def kernel(x_prompt, x_sample, cache_attn_k, cache_attn_v, state_hgrn, p_prompt, p_sample,
           rel_bias_table, ffn1_pre, ffn1_post, ffn1_w_gate, ffn1_w_up, ffn1_w_down,
           mix_pre, mix_post, w_in, w_out, attn_sinks, hgrn_lb_logits, hgrn_norm,
           ffn2_pre, ffn2_post, ffn2_w_gate, ffn2_w_up, ffn2_w_down,
           ple_pre, ple_post, w_ple_gate, w_ple_proj):
    f = lambda a: np.ascontiguousarray(np.asarray(a, dtype=np.float32))
    x_prompt, x_sample, p_prompt, p_sample = f(x_prompt), f(x_sample), f(p_prompt), f(p_sample)
    B, SEQ, _ = x_prompt.shape
    HALF = SEQ // 2
    NPT = HALF // T
    consts = host_consts(rel_bias_table, [ffn1_pre[0], ffn1_post[0], mix_pre[0], mix_post[0], ffn2_pre[0], ffn2_post[0], ple_pre[0], ple_post[0]],
                         hgrn_norm[0], np.asarray(hgrn_lb_logits), np.asarray(attn_sinks)[0])
    wts = dict(w1g=f(ffn1_w_gate[0]), w1u=f(ffn1_w_up[0]), w1d=f(ffn1_w_down[0]), win=f(w_in[0]), wout=f(w_out[0]),
               w2g=f(ffn2_w_gate[0]), w2u=f(ffn2_w_up[0]), w2d=f(ffn2_w_down[0]), wpg=f(w_ple_gate[0]), wpp=f(w_ple_proj[0]))
    ck = f(cache_attn_k[0]).reshape(32, 128, 256)
    cv = f(cache_attn_v[0]).reshape(32, 128, 256)
    st = f(state_hgrn[0])
    nc = build(NPT, NPRE=NPT)
    in_maps = []
    for c in range(NCORES):
        b, hf = c // 2, c % 2
        m = dict(xp=x_prompt[b, hf * HALF:(hf + 1) * HALF], pp=p_prompt[0, b, hf * HALF:(hf + 1) * HALF], xpre=x_prompt[b, 0:HALF],
                 flag=np.full((128, 1), float(hf), np.float32),
                 xs=x_sample[4 * c:4 * c + 4].reshape(T, D), ps=p_sample[0, 4 * c:4 * c + 4].reshape(T, 256),
                 ck=ck[4 * c:4 * c + 4], cv=cv[4 * c:4 * c + 4], st=st[4 * c:4 * c + 4])
        m.update(wts)
        m.update(consts)
        in_maps.append(m)
    res = run_bass_kernel_spmd(nc, in_maps, core_ids=list(range(NCORES)))
    R = list(res.results)
    while len(R) < 8:
        R.append(R[0])
    yp = np.stack([np.concatenate([R[2 * b]["yp"], R[2 * b + 1]["yp"]], axis=0) for b in range(B)]).reshape(B, SEQ, D)
    ys = np.concatenate([R[c]["ys"].reshape(4, 64, D) for c in range(8)], axis=0)
    kp = np.stack([R[2 * b + 1]["kp"].reshape(128, 2, 128) for b in range(B)])[None]
    vp = np.stack([R[2 * b + 1]["vp"].reshape(128, 2, 128) for b in range(B)])[None]
    sp = np.stack([R[2 * b + 1]["spo"] for b in range(B)])[None]
    ks = np.concatenate([R[c]["kso"].reshape(4, 128, 2, 128) for c in range(8)], axis=0)[None]
    vs = np.concatenate([R[c]["vso"].reshape(4, 128, 2, 128) for c in range(8)], axis=0)[None]
    ss = np.concatenate([R[c]["sso"] for c in range(8)], axis=0)[None]
    if DEBUG_MIX:
        global LAST_DBG
        LAST_DBG = np.asarray(R[0]["dbgmix"]).astype(np.float32)
    return tuple(np.ascontiguousarray(a.astype(np.float32)) for a in (yp, ys, kp, vp, sp, ks, vs, ss))
```
